# Optimizing a Trainium2 kernel written in Bass

```python
import math
import jax, jax.numpy as jnp
from jax import lax
import numpy as np

D_MODEL = 2048
BATCH = 1
SEQ = 8192
DEPTH = 4

N_MIXERS = 2
GRID_W = 64
HEAD_DIM = 128
N_HEADS = D_MODEL // HEAD_DIM
N_KV_HEADS = 4
GROUP = N_HEADS // N_KV_HEADS
Q_DIM = N_HEADS * HEAD_DIM
KV_DIM = N_KV_HEADS * HEAD_DIM
QKV_DIM = Q_DIM + 2 * KV_DIM
Q_BLOCK = 128
ROPE_THETA = 10000.0
ROPE_FREQS = HEAD_DIM // 4
D_RNN = D_MODEL
RNN_BLOCK = 128
N_RNN_BLOCKS = D_RNN // RNN_BLOCK
CONV_W = 4
CONV_PAD_L = 2
C_DECAY = 8.0
D_FF = 5632
NORM_EPS = 1e-6

kernel_name = "hybrid_gqa2drope_bidir_rglru_macaron"


def rms_norm(x, g):
    xf = x.astype(jnp.float32)
    y = xf * lax.rsqrt(jnp.mean(xf * xf, axis=-1, keepdims=True) + NORM_EPS)
    return (y * g.astype(jnp.float32)).astype(x.dtype)


def swiglu(xn, w_gu, w_down):
    gu = xn @ w_gu
    g, u = jnp.split(gu, 2, axis=-1)
    return (jax.nn.silu(g) * u) @ w_down


def axial_angles(S):
    rows_n = S // GRID_W
    rows = jnp.repeat(jnp.arange(rows_n, dtype=jnp.float32), GRID_W)
    cols = jnp.tile(jnp.arange(GRID_W, dtype=jnp.float32), rows_n)
    inv_freq = ROPE_THETA ** (-jnp.arange(ROPE_FREQS, dtype=jnp.float32) / ROPE_FREQS)
    return rows[:, None] * inv_freq[None, :], cols[:, None] * inv_freq[None, :]


def rope_half(xh, ang):
    x1, x2 = jnp.split(xh, 2, axis=-1)
    c, s = jnp.cos(ang), jnp.sin(ang)
    return jnp.concatenate([x1 * c - x2 * s, x2 * c + x1 * s], axis=-1)


def axial_rope(x, ang_row, ang_col):
    xf = x.astype(jnp.float32)
    xr, xc = jnp.split(xf, 2, axis=-1)
    out = jnp.concatenate([rope_half(xr, ang_row), rope_half(xc, ang_col)], axis=-1)
    return out.astype(x.dtype)


def attention_mixer(xn, w_qkv, q_gain, k_gain, w_o):
    B, S, _ = xn.shape
    qkv = xn @ w_qkv
    q = qkv[..., :Q_DIM].reshape(B, S, N_KV_HEADS, GROUP, HEAD_DIM)
    k = qkv[..., Q_DIM:Q_DIM + KV_DIM].reshape(B, S, N_KV_HEADS, HEAD_DIM)
    v = qkv[..., Q_DIM + KV_DIM:].reshape(B, S, N_KV_HEADS, HEAD_DIM)
    q = rms_norm(q, q_gain)
    k = rms_norm(k, k_gain)
    ang_row, ang_col = axial_angles(S)
    q = axial_rope(q, ang_row[None, :, None, None, :], ang_col[None, :, None, None, :])
    k = axial_rope(k, ang_row[None, :, None, :], ang_col[None, :, None, :])
    scale = HEAD_DIM ** -0.5
    n_qb = S // Q_BLOCK
    qb = q.reshape(B, n_qb, Q_BLOCK, N_KV_HEADS, GROUP, HEAD_DIM).swapaxes(0, 1)

    def one_block(q_blk):
        s = jnp.einsum('bqkgd,bskd->bkgqs', q_blk, k).astype(jnp.float32) * scale
        p = jax.nn.softmax(s, axis=-1).astype(v.dtype)
        return jnp.einsum('bkgqs,bskd->bqkgd', p, v)

    o = lax.map(one_block, qb)
    o = o.swapaxes(0, 1).reshape(B, S, Q_DIM)
    return o @ w_o


def centred_depthwise_conv(x, w, b):
    S = x.shape[1]
    xp = jnp.pad(x, ((0, 0), (CONV_PAD_L, CONV_W - 1 - CONV_PAD_L), (0, 0)))
    y = b
    for tap in range(CONV_W):
        y = y + xp[:, tap:tap + S, :] * w[tap]
    return y


def _lin_combine(e1, e2):
    a1, b1 = e1
    a2, b2 = e2
    return a1 * a2, a2 * b1 + b2


def rg_lru(xc, gate_w, gate_b, lam, reverse):
    B, S, _ = xc.shape
    xf = xc.astype(jnp.float32)
    xb = xf.reshape(B, S, N_RNN_BLOCKS, RNN_BLOCK)
    g = jnp.einsum('bsnc,zncd->zbsnd', xb, gate_w.astype(jnp.float32)).reshape(2, B, S, D_RNN)
    g = g + gate_b.astype(jnp.float32)[:, None, None, :]
    r_gate = jax.nn.sigmoid(g[0])
    i_gate = jax.nn.sigmoid(g[1])
    log_a = -C_DECAY * r_gate * jax.nn.softplus(-lam.astype(jnp.float32))
    a = jnp.exp(log_a)
    b = jnp.sqrt(-jnp.expm1(2.0 * log_a)) * (i_gate * xf)
    _, h = lax.associative_scan(_lin_combine, (a, b), axis=1, reverse=reverse)
    return h


def recurrent_mixer(xn, w_in, conv_w, conv_b, gate_w, gate_b, lam, w_out):
    u = xn @ w_in
    gate_branch, rec_branch = jnp.split(u, 2, axis=-1)
    xc = centred_depthwise_conv(rec_branch, conv_w, conv_b)
    h = (rg_lru(xc, gate_w[0], gate_b[0], lam[0], reverse=False)
         + rg_lru(xc, gate_w[1], gate_b[1], lam[1], reverse=True))
    y = jax.nn.gelu(gate_branch) * h.astype(xn.dtype)
    return y @ w_out


def setup_inputs(seed: int = 0) -> dict:
    key = jax.random.key(seed)
    ks = jax.random.split(key, 20)
    n_attn = len(range(0, DEPTH, N_MIXERS))
    n_rec = DEPTH - n_attn
    f32 = jnp.float32

    def nrm(k, shape, fan_in):
        return jax.random.normal(k, shape, f32) * (fan_in ** -0.5)

    def gain(k, shape):
        return 1.0 + 0.02 * jax.random.normal(k, shape, f32)

    x = jax.random.normal(ks[0], (BATCH, SEQ, D_MODEL), f32)
    ffn_norm = gain(ks[1], (DEPTH, 2, D_MODEL))
    ffn_w_gu = nrm(ks[2], (DEPTH, 2, D_MODEL, 2 * D_FF), D_MODEL)
    ffn_w_down = nrm(ks[3], (DEPTH, 2, D_FF, D_MODEL), D_FF)
    attn_norm = gain(ks[4], (n_attn, D_MODEL))
    attn_w_qkv = nrm(ks[5], (n_attn, D_MODEL, QKV_DIM), D_MODEL)
    attn_q_norm = gain(ks[6], (n_attn, HEAD_DIM))
    attn_k_norm = gain(ks[7], (n_attn, HEAD_DIM))
    attn_w_o = nrm(ks[8], (n_attn, Q_DIM, D_MODEL), Q_DIM)
    rec_norm = gain(ks[9], (n_rec, D_MODEL))
    rec_w_in = nrm(ks[10], (n_rec, D_MODEL, 2 * D_RNN), D_MODEL)
    rec_conv_w = nrm(ks[11], (n_rec, CONV_W, D_RNN), CONV_W)
    rec_conv_b = 0.01 * jax.random.normal(ks[12], (n_rec, D_RNN), f32)
    rec_gate_w = nrm(ks[13], (n_rec, 2, 2, N_RNN_BLOCKS, RNN_BLOCK, RNN_BLOCK), RNN_BLOCK)
    rec_gate_b = 0.01 * jax.random.normal(ks[14], (n_rec, 2, 2, D_RNN), f32)
    a8 = jax.random.uniform(ks[15], (n_rec, 2, D_RNN), f32, minval=0.9, maxval=0.999)
    s = a8 ** (1.0 / C_DECAY)
    rec_lambda = jnp.log(s) - jnp.log1p(-s)
    rec_w_out = nrm(ks[16], (n_rec, D_RNN, D_MODEL), D_RNN)
    final_norm = gain(ks[17], (D_MODEL,))
    return {
        "x": x,
        "ffn_norm": ffn_norm, "ffn_w_gu": ffn_w_gu, "ffn_w_down": ffn_w_down,
        "attn_norm": attn_norm, "attn_w_qkv": attn_w_qkv,
        "attn_q_norm": attn_q_norm, "attn_k_norm": attn_k_norm, "attn_w_o": attn_w_o,
        "rec_norm": rec_norm, "rec_w_in": rec_w_in, "rec_conv_w": rec_conv_w,
        "rec_conv_b": rec_conv_b, "rec_gate_w": rec_gate_w, "rec_gate_b": rec_gate_b,
        "rec_lambda": rec_lambda, "rec_w_out": rec_w_out,
        "final_norm": final_norm,
    }


def reference(x, ffn_norm, ffn_w_gu, ffn_w_down, attn_norm, attn_w_qkv, attn_q_norm,
              attn_k_norm, attn_w_o, rec_norm, rec_w_in, rec_conv_w, rec_conv_b,
              rec_gate_w, rec_gate_b, rec_lambda, rec_w_out, final_norm):
    for i in range(DEPTH):
        x = x + 0.5 * swiglu(rms_norm(x, ffn_norm[i, 0]), ffn_w_gu[i, 0], ffn_w_down[i, 0])
        j = i // N_MIXERS
        if i % N_MIXERS == 0:
            x = x + attention_mixer(rms_norm(x, attn_norm[j]), attn_w_qkv[j],
                                    attn_q_norm[j], attn_k_norm[j], attn_w_o[j])
        else:
            x = x + recurrent_mixer(rms_norm(x, rec_norm[j]), rec_w_in[j], rec_conv_w[j],
                                    rec_conv_b[j], rec_gate_w[j], rec_gate_b[j],
                                    rec_lambda[j], rec_w_out[j])
        x = x + 0.5 * swiglu(rms_norm(x, ffn_norm[i, 1]), ffn_w_gu[i, 1], ffn_w_down[i, 1])
    return rms_norm(x, final_norm)
```

```python
import contextlib
import math
import numpy as np
import ml_dtypes
import concourse.bass as bass
import concourse.mybir as mybir
from concourse.bass_utils import run_bass_kernel_spmd

F32 = mybir.dt.float32
BF16 = mybir.dt.bfloat16
AF = mybir.ActivationFunctionType
ALU = mybir.AluOpType
AX = mybir.AxisListType

NCORES = 8
D = 2048
S = 8192
T = S // NCORES
KC = D // 128
DFF = 5632
JC = DFF // 128
NQ = 4
JQ = JC // NQ
DEPTH = 4
EPS = 1e-6
ENGS = ("pe", "act", "dve", "pool", "sp")

C_GAIN = 0
C_AQK = 208
C_CONV = 212
C_GB = 372
C_LAM = 500
C_MASK = 564
C_EPS = 612
C_ONE = 613
C_ZERO = 614
NCV = 616

ARENA_BYTES = 96 * 1024


class Res:
    __slots__ = ("w", "r")

    def __init__(self):
        self.w = {}
        self.r = {}


class Prog:
    def __init__(self):
        self.nc = bass.Bass("TRN2", target_bir_lowering=False)
        self.stack = contextlib.ExitStack()
        self.streams = {e: [] for e in ENGS}
        self.cnt = {e: 0 for e in ENGS}
        self.pending = {e: False for e in ENGS}
        self.res = {}
        self.seen = {e: {} for e in ENGS}
        self.dma_cnt = {}
        self.sems = {}
        self.out_tokens = {}
        self.fence_tok = None
        self.n_ops = 0

    def sbuf(self, name, shape, dt):
        return self.stack.enter_context(self.nc.sbuf_tensor(name, list(shape), dt))

    def psum(self, name, shape, dt=F32):
        return self.stack.enter_context(self.nc.psum_tensor(name, list(shape), dt))

    def _need(self, eng, tok_map, waits):
        for sk, val in tok_map.items():
            if sk == "E:pe" and eng == "pe":
                continue
            if self.seen[eng].get(sk, 0) >= val:
                continue
            if waits.get(sk, 0) < val:
                waits[sk] = val

    def op(self, eng, fn, reads=(), writes=(), inc=True, dma=None, out=False, dinc=16):
        waits = {}
        if self.fence_tok is not None:
            self._need(eng, self.fence_tok, waits)
        for r in reads:
            rs = self.res.get(r)
            if rs is not None:
                self._need(eng, rs.w, waits)
        for w in writes:
            rs = self.res.get(w)
            if rs is not None:
                self._need(eng, rs.w, waits)
                self._need(eng, rs.r, waits)
        for sk, val in waits.items():
            self.seen[eng][sk] = val
        if dma is not None:
            sk = "D:" + str(dma)
            self.dma_cnt[sk] = self.dma_cnt.get(sk, 0) + dinc
            tok = (sk, self.dma_cnt[sk])
            inc = False
        else:
            sk = "E:" + eng
            if inc:
                self.cnt[eng] += 1
                tok = (sk, self.cnt[eng])
                self.pending[eng] = False
            else:
                tok = (sk, self.cnt[eng] + 1)
                self.pending[eng] = True
        for r in reads:
            rs = self.res.setdefault(r, Res())
            if rs.r.get(tok[0], 0) < tok[1]:
                rs.r[tok[0]] = tok[1]
        for w in writes:
            rs = self.res.setdefault(w, Res())
            rs.w = {tok[0]: tok[1]}
            rs.r = {}
        if out:
            if self.out_tokens.get(tok[0], 0) < tok[1]:
                self.out_tokens[tok[0]] = tok[1]
        self.streams[eng].append((fn, waits, inc, dma, dinc))
        self.n_ops += 1
        return tok

    def fence(self, fn):
        for e in ENGS:
            assert not self.pending[e], f"fence with pending ops on {e}"
        waits = {}
        allt = {}
        for e in ENGS:
            if self.cnt[e] > 0:
                allt["E:" + e] = self.cnt[e]
        for sk, v in self.dma_cnt.items():
            allt[sk] = v
        self._need("act", allt, waits)
        for sk, val in waits.items():
            self.seen["act"][sk] = val
        self.cnt["act"] += 1
        tok = ("E:act", self.cnt["act"])
        self.streams["act"].append((fn, waits, True, None, 16))
        self.fence_tok = {tok[0]: tok[1]}
        self.res = {}

    def _sem(self, sk):
        s = self.sems.get(sk)
        if s is None:
            nm = "s_" + "".join(ch if ch.isalnum() else "_" for ch in sk)
            s = self.stack.enter_context(self.nc.semaphore(nm))
            self.sems[sk] = s
        return s

    def finish(self):
        nc = self.nc
        for e in ENGS:
            assert not self.pending[e], f"engine {e} has pending un-inc'd ops"
        fin = dict(self.out_tokens)
        allsk = set()
        for e in ENGS:
            for (_, waits, inc, dma, dinc) in self.streams[e]:
                allsk.update(waits.keys())
                if dma is not None:
                    allsk.add("D:" + str(dma))
            allsk.add("E:" + e)
        allsk.update(fin.keys())
        for sk in sorted(allsk):
            self._sem(sk)
        block = self.stack.enter_context(nc.Block())
        engobj = {"pe": "tensor", "act": "scalar", "dve": "vector", "pool": "gpsimd", "sp": "sync"}

        def make(e):
            def body(eng):
                for (fn, waits, inc, dma, dinc) in self.streams[e]:
                    for sk, val in waits.items():
                        eng.wait_ge(self.sems[sk], val)
                    ins = fn(eng)
                    if dma is not None:
                        ins.then_inc(self.sems["D:" + str(dma)], dinc)
                    elif inc:
                        ins.then_inc(self.sems["E:" + e], 1)
                if e == "sp":
                    for sk, val in fin.items():
                        eng.wait_ge(self.sems[sk], val)
            return body

        for e in ENGS:
            getattr(block, engobj[e])(make(e))
        self.stack.close()
        return nc


class Item:
    def __init__(self, fn, pool=None, src=None, kind="compute", **kw):
        self.fn = fn
        self.pool = pool
        self.src = src
        self.kind = kind
        self.kw = kw


class Builder:
    def __init__(self, fused):
        self.fused = fused
        self.P = Prog()
        self.nc = self.P.nc
        self.in_decl = {}
        self.out_decl = {}
        self.scratch = {}
        P = self.P
        self.xT = P.sbuf("xT", [128, KC * T], F32)
        self.xn = P.sbuf("xn", [128, KC * T], BF16)
        self.cvec = P.sbuf("cvec_sb", [128, NCV], F32)
        self.ones_f = P.sbuf("ones_f", [128, 128], F32)
        self.ones_b = P.sbuf("ones_b", [128, 128], BF16)
        self.prot = P.sbuf("prot_sb", [128, 128], F32)
        self.sq = [P.sbuf(f"sq{i}", [128, 512], F32) for i in range(2)]
        self.rs = [P.sbuf(f"rs{i}", [128, 512], F32) for i in range(2)]
        self.nbias = P.sbuf("nbias", [128, 2], F32)
        self.small = P.sbuf("small", [128, 64], F32)
        self.sm2 = P.sbuf("sm2", [128, 256], F32)
        self.rowt = P.sbuf("rowt", [1, 256], F32)
        self.arena = P.sbuf("arena", [128, ARENA_BYTES // 2], BF16)
        self.B = [P.psum(f"B{i}", [128, 512], F32) for i in range(8)]
        self.x3 = self.xT[:, :].rearrange("p (k t) -> p k t", t=T)
        self.xn3 = self.xn[:, :].rearrange("p (k t) -> p k t", t=T)
        self.pools = {}
        self.const_loaded = False

    def din(self, name, shape, dt=F32):
        if name not in self.in_decl:
            self.in_decl[name] = self.nc.dram_tensor(name, list(shape), dt, kind="ExternalInput")
        return self.in_decl[name]

    def dout(self, name, shape, dt=F32):
        if name not in self.out_decl:
            self.out_decl[name] = self.nc.dram_tensor(name, list(shape), dt, kind="ExternalOutput")
        return self.out_decl[name]

    def dscr(self, name, shape, dt, role):
        if self.fused:
            if name not in self.scratch:
                self.scratch[name] = self.nc.dram_tensor(name, list(shape), dt)
            return self.scratch[name]
        if role == "w":
            return self.dout(name, shape, dt)
        return self.din(name, shape, dt)

    def carve(self, off_bytes, nbytes, dt):
        assert off_bytes % 4 == 0 and off_bytes + nbytes <= ARENA_BYTES, (off_bytes, nbytes)
        v = self.arena[:, off_bytes // 2:(off_bytes + nbytes) // 2]
        if dt == F32:
            v = v.bitcast(F32)
        return v

    def mm(self, out, lhsT, rhs, st, sp, R, W, inc):
        self.P.op("pe", lambda e: e.matmul(out, lhsT, rhs, start=st, stop=sp), R, W, inc=inc)

    def act(self, out, in_, func, R, W, scale=1.0, bias=None, accum=None):
        kw = {}
        if bias is not None:
            kw["bias"] = bias
        if accum is not None:
            kw["accum_out"] = accum
        self.P.op("act", lambda e: e.activation(out=out, in_=in_, func=func, scale=scale, **kw), R, W)

    def stt(self, eng, out, in0, scalar, in1, op0, op1, R, W):
        self.P.op(eng, lambda e: e.scalar_tensor_tensor(out=out, in0=in0, scalar=scalar, in1=in1, op0=op0, op1=op1), R, W)

    def ts(self, eng, out, in0, s1, s2, op0, op1, R, W):
        if s2 is None:
            self.P.op(eng, lambda e: e.tensor_scalar(out=out, in0=in0, scalar1=s1, scalar2=None, op0=op0), R, W)
        else:
            self.P.op(eng, lambda e: e.tensor_scalar(out=out, in0=in0, scalar1=s1, scalar2=s2, op0=op0, op1=op1), R, W)

    def tt(self, eng, out, in0, in1, op, R, W):
        self.P.op(eng, lambda e: e.tensor_tensor(out=out, in0=in0, in1=in1, op=op), R, W)

    def cp(self, eng, out, in_, R, W):
        if eng == "act":
            self.P.op("act", lambda e: e.activation(out=out, in_=in_, func=AF.Copy), R, W)
        else:
            self.P.op(eng, lambda e: e.tensor_copy(out=out, in_=in_), R, W)

    def recip(self, out, in_, R, W):
        self.P.op("dve", lambda e: e.reciprocal(out=out, in_=in_), R, W)

    def dma(self, q, out, in_, R, W, chan, out_flag=False, **kw):
        self.P.op(q, lambda e: e.dma_start(out=out, in_=in_, **kw), R, W, dma=chan, out=out_flag)

    def cv(self, col, n=1):
        return self.cvec[:, col:col + n]

    def load_consts(self):
        cv = self.din("cvec", [128, NCV])
        pr = self.din("prot", [128, 128])
        self.dma("sp", self.cvec[:, :], cv.ap(), [], ["cvec"], "cst")
        self.dma("sp", self.prot[:, :], pr.ap(), [], ["prot"], "cst")
        self.P.op("dve", lambda e: e.memset(self.ones_f[:, :], 1.0), [], ["ones_f"])
        self.P.op("dve", lambda e: e.memset(self.ones_b[:, :], 1.0), [], ["ones_b"])

    def rmsnorm(self, gi):
        x3, xn3 = self.x3, self.xn3
        for th in range(2):
            tsl = slice(th * 512, (th + 1) * 512)
            bank = self.B[6 + th]
            for k in range(KC):
                sq = self.sq[k % 2]
                self.act(sq[:, :], x3[:, k, tsl], AF.Square, [("x", k, th)], [("sq", k % 2)])
                self.mm(bank[:, :], self.ones_f[:, :], sq[:, :], k == 0, k == KC - 1,
                        [("sq", k % 2), "ones_f"], [("B", 6 + th)], inc=True)
            rt = self.rs[th]
            self.act(rt[:, :], bank[:, :], AF.Sqrt, [("B", 6 + th), "cvec"], [("rs", th)],
                     scale=1.0 / D, bias=self.cv(C_EPS))
            self.recip(rt[:, :], rt[:, :], [("rs", th)], [("rs", th)])
            for k in range(KC):
                self.stt("dve", xn3[:, k, tsl], x3[:, k, tsl], self.cv(C_GAIN + gi * 16 + k), rt[:, :],
                         ALU.mult, ALU.mult, [("x", k, th), ("rs", th), "cvec"], [("xn", k, th)])

    def make_pool(self, name, off=None, slot_bytes=None, nslots=None, queue="pool", views=None):
        if views is None:
            views = [self.carve(off + i * slot_bytes, slot_bytes, BF16) for i in range(nslots)]
        self.pools[name] = dict(views=views, n=len(views), cnt=0, q=queue, last_item={})

    def slot_view(self, name, s):
        return self.pools[name]["views"][s]

    def ffn_items(self, l, f, items):
        gi = l * 2 + f
        wgu = lambda: self.din(f"wgu_{l}_{f}", [JC, 128, KC * 256])
        wd = lambda: self.din(f"wd_{l}_{f}", [NQ, KC, 128, JQ * 128])
        A_H = 0
        A_WGU = A_H + JQ * T * 2
        A_WD = A_WGU + 3 * 8192
        A_SG = A_WD + 3 * JQ * 256
        A_END = A_SG + 4 * 2048

        def setup(_):
            self.make_pool("wgu", A_WGU, 8192, 3)
            self.make_pool("wd", A_WD, JQ * 256, 3)
            self.rmsnorm(gi)
        items.append(Item(setup))
        h3 = lambda: self.carve(A_H, JQ * T * 2, BF16).rearrange("p (j t) -> p j t", t=T)
        sgt = lambda i: self.carve(A_SG + i * 2048, 2048, F32)
        assert A_END <= ARENA_BYTES

        for q in range(NQ):
            for jj in range(JQ):
                j = q * JQ + jj

                def gu(slot, tile, jj=jj, j=j):
                    w3 = tile.rearrange("p (k c) -> p k c", c=256)
                    for half in range(2):
                        for k in range(KC):
                            for th in range(2):
                                self.mm(self.B[half * 2 + th][:, :], w3[:, k, half * 128:(half + 1) * 128],
                                        self.xn3[:, k, th * 512:(th + 1) * 512], k == 0, k == KC - 1,
                                        [("wgu", slot), ("xn", k, th)], [("B", half * 2 + th)], inc=(k == KC - 1))
                    for th in range(2):
                        sg = sgt((jj % 2) * 2 + th)
                        self.act(sg, self.B[th][:, :], AF.Silu, [("B", th)], [("sg", (jj % 2) * 2 + th)])
                        self.tt("dve", h3()[:, jj, th * 512:(th + 1) * 512], self.B[2 + th][:, :], sg, ALU.mult,
                                [("B", 2 + th), ("sg", (jj % 2) * 2 + th)], [("h", jj, th)])
                items.append(Item(gu, pool="wgu",
                                  src=lambda j=j: [(lambda tl: tl.rearrange("p (a b) -> p a b", b=2048),
                                                    wgu().ap()[j].rearrange("p (a b) -> p a b", b=2048))]))
            for m in range(KC):
                def down(slot, tile, m=m):
                    w3 = tile.rearrange("p (j c) -> p j c", c=128)
                    b0 = 4 + (m % 2) * 2
                    for jj in range(JQ):
                        for th in range(2):
                            self.mm(self.B[b0 + th][:, :], w3[:, jj, :], h3()[:, jj, th * 512:(th + 1) * 512],
                                    jj == 0, jj == JQ - 1, [("wd", slot), ("h", jj, th)], [("B", b0 + th)],
                                    inc=(jj == JQ - 1))
                    for th in range(2):
                        xs = self.x3[:, m, th * 512:(th + 1) * 512]
                        self.stt("dve", xs, self.B[b0 + th][:, :], 0.5, xs, ALU.mult, ALU.add,
                                 [("B", b0 + th), ("x", m, th)], [("x", m, th)])
                items.append(Item(down, pool="wd",
                                  src=lambda q=q, m=m: [(lambda tl: tl, wd().ap()[q, m])]))
        items.append(Item(None, kind="fence"))


    def attn_items(self, a, items):
        gi = 8 + a
        A_QO, A_KLOC, A_VLOC, A_WV, A_CT, A_W128, A_WK = 0, 32768, 40960, 49152, 65536, 73728, 86016
        qo3 = lambda: self.carve(A_QO, 32768, BF16).rearrange("p (h t) -> p h t", t=T)
        kloc3 = lambda: self.carve(A_KLOC, 8192, BF16).rearrange("p (g t) -> p g t", t=T)
        vloc3 = lambda: self.carve(A_VLOC, 8192, BF16).rearrange("p (tt c) -> p tt c", c=512)
        ctab = lambda: self.carve(A_CT, 8192, F32)
        wk = lambda th, i: self.carve(A_WK + (th * 3 + i) * 2048, 2048, F32)
        scale = 128.0 ** -0.5
        cut_id = f"a{a}"

        def setup(_):
            self.make_pool("w128", A_W128, 4096, 3)
            self.make_pool("wv", A_WV, 8192, 2)
            self.rmsnorm(gi)
            ct = self.din("ctab", [128, 2048])
            self.dma("sp", ctab(), ct.ap(), [], ["ctab"], "cst")
            ar = self.din("arow", [2, 256])
            self.dma("sp", self.rowt[0:1, :], ar.ap()[a:a + 1, :], [], ["rowt"], "cst")
            sm = self.small
            self.P.op("dve", lambda e: e.tensor_reduce(out=sm[0:1, 2:4], in_=self.rowt[0:1, :].rearrange("p (w d) -> p w d", d=128),
                                                      axis=AX.X, op=ALU.max, apply_absolute_value=True), ["rowt"], ["sm_a"])
            self.tt("dve", sm[0:1, 4:5], sm[0:1, 2:3], sm[0:1, 3:4], ALU.mult, ["sm_a"], ["sm_b"])
            self.ts("dve", sm[0:1, 6:7], sm[0:1, 4:5], -math.sqrt(128.0), None, ALU.mult, None, ["sm_b"], ["sm_c"])
            self.ts("dve", sm[0:1, 7:8], sm[0:1, 4:5], -math.sqrt(128.0), None, ALU.mult, None, ["sm_b"], ["sm_d"])
            self.mm(self.B[5][:, 0:2], self.ones_f[0:1, :], sm[0:1, 6:8], True, True, ["sm_c", "sm_d", "ones_f"], [("B", 5)], inc=True)
            self.cp("act", self.nbias[:, a:a + 1], self.B[5][:, 0:1], [("B", 5)], ["nbias"])
        items.append(Item(setup))

        wqk = lambda: self.din(f"wqk_{a}", [20, 128, KC * 128])
        wv = lambda: self.din(f"wv_{a}", [2, 128, KC * 256])
        wo = lambda: self.din(f"wo_{a}", [KC, 128, KC * 128])

        for c20 in range(20):
            def qk(slot, tile, c20=c20):
                w3 = tile.rearrange("p (k c) -> p k c", c=128)
                gcol = C_AQK + a * 2 + (0 if c20 < 16 else 1)
                for th in range(2):
                    tsl = slice(th * 512, (th + 1) * 512)
                    bp = self.B[th]
                    for k in range(KC):
                        self.mm(bp[:, :], w3[:, k, :], self.xn3[:, k, tsl], k == 0, k == KC - 1,
                                [("w128", slot), ("xn", k, th)], [("B", th)], inc=(k == KC - 1))
                    sq = self.sq[th]
                    self.act(sq[:, :], bp[:, :], AF.Square, [("B", th)], [("sq", th)])
                    self.mm(self.B[2 + th][:, :], self.ones_f[:, :], sq[:, :], True, True, [("sq", th), "ones_f"], [("B", 2 + th)], inc=True)
                    rt = self.rs[th]
                    self.act(rt[:, :], self.B[2 + th][:, :], AF.Sqrt, [("B", 2 + th), "cvec"], [("rs", th)],
                             scale=1.0 / 128, bias=self.cv(C_EPS))
                    self.recip(rt[:, :], rt[:, :], [("rs", th)], [("rs", th)])
                    qn, t1, t2 = wk(th, 0), wk(th, 1), wk(th, 2)
                    self.stt("dve", qn, bp[:, :], self.cv(gcol), rt[:, :], ALU.mult, ALU.mult,
                             [("B", th), ("rs", th), "cvec"], [("wk", th, 0)])
                    self.mm(self.B[4 + th][:, :], self.prot[:, :], qn, True, True, [("wk", th, 0), "prot"], [("B", 4 + th)], inc=True)
                    self.tt("pool", t1, qn, ctab()[:, th * 512:(th + 1) * 512], ALU.mult, [("wk", th, 0), "ctab"], [("wk", th, 1)])
                    self.tt("dve", t2, self.B[4 + th][:, :], ctab()[:, 1024 + th * 512:1024 + (th + 1) * 512], ALU.mult,
                            [("B", 4 + th), "ctab"], [("wk", th, 2)])
                    if c20 < 16:
                        dest, dres = qo3()[:, c20, tsl], ("qo", c20, th)
                    else:
                        dest, dres = kloc3()[:, c20 - 16, tsl], ("kloc", c20 - 16, th)
                    self.tt("pool", dest, t1, t2, ALU.add, [("wk", th, 1), ("wk", th, 2)], [dres])
            items.append(Item(qk, pool="w128", src=lambda c20=c20: [(lambda tl: tl, wqk().ap()[c20])]))

        for vh in range(2):
            def vproj(slot, tile, vh=vh):
                w3 = tile.rearrange("p (k c) -> p k c", c=256)
                for tt_ in range(8):
                    bank = self.B[6 + tt_ % 2]
                    for k in range(KC):
                        self.mm(bank[:, 0:256], self.xn3[:, k, tt_ * 128:(tt_ + 1) * 128], w3[:, k, :], k == 0, k == KC - 1,
                                [("wv", slot), ("xn", k, tt_ // 4)], [("B", 6 + tt_ % 2)], inc=(k == KC - 1))
                    self.cp("act", vloc3()[:, tt_, vh * 256:(vh + 1) * 256], bank[:, 0:256], [("B", 6 + tt_ % 2)], [("vloc", tt_, vh)])
            items.append(Item(vproj, pool="wv",
                              src=lambda vh=vh: [(lambda tl: tl.rearrange("p (a b) -> p a b", b=2048),
                                                  wv().ap()[vh].rearrange("p (a b) -> p a b", b=2048))]))

        def kvout(_):
            kin = self.dscr(f"kin_{a}", [512, T], BF16, "w")
            vin = self.dscr(f"vin_{a}", [T, 512], BF16, "w")
            self.dma("sp", kin.ap().rearrange("(g d) t -> d g t", d=128), kloc3(),
                     [("kloc", g, th) for g in range(4) for th in range(2)], ["kin"], "xo0", out_flag=not self.fused)
            self.dma("sp", vin.ap().rearrange("(tt p) c -> p tt c", p=128), vloc3(),
                     [("vloc", t_, vh) for t_ in range(8) for vh in range(2)], ["vin"], "xo1", out_flag=not self.fused)
        items.append(Item(kvout))
        items.append(Item(None, kind="cut", cid=cut_id,
                          gathers=[(f"kin_{a}", [512, T], f"kall_{a}", [NCORES * 512, T], BF16, "kin", "kall"),
                                   (f"vin_{a}", [T, 512], f"vall_{a}", [S, 512], BF16, "vin", "vall")],
                          save=["x", "qo"]))

        A_KV1, A_W128b, A_PT, A_RL = 32768, 65536, 77824, 81920
        pT = lambda i: self.carve(A_PT + i * 1024, 1024, BF16)
        rl = lambda i: self.carve(A_RL + i * 2048, 2048, F32)

        def setup2(_):
            self.make_pool("w128", A_W128b, 4096, 3)
            views = [(self.xn[:, 0:8192], self.xn[:, 8192:16384].rearrange("p (tl d) -> p tl d", d=128)),
                     (self.carve(A_KV1, 16384, BF16), self.carve(A_KV1 + 16384, 16384, BF16).rearrange("p (tl d) -> p tl d", d=128))]
            self.make_pool("kv", views=views, queue="sp")
        items.append(Item(setup2))

        def kvsrc(g):
            kall = self.dscr(f"kall_{a}", [NCORES * 512, T], BF16, "r")
            vall = self.dscr(f"vall_{a}", [S, 512], BF16, "r")
            lst = [(lambda tl: tl[0].rearrange("d (r t) -> d r t", t=T),
                    kall.ap().rearrange("(r g d) t -> d r g t", g=4, d=128)[:, :, g, :])]
            vsrc = vall.ap().rearrange("(tl p) c -> p tl c", p=128)
            for i in range(4):
                lst.append((lambda tl, i=i: tl[1][:, i * 16:(i + 1) * 16, :], vsrc[:, i * 16:(i + 1) * 16, g * 128:(g + 1) * 128]))
            return lst

        for g in range(4):
            def core(slot, tile, g=g):
                kT, v3 = tile
                steps = [(hq, qb, kt) for hq in range(4) for qb in range(2) for kt in range(64)]
                n = len(steps)
                SK = 2

                def Sstep(i):
                    hq, qb, kt = steps[i]
                    h = g * 4 + hq
                    self.mm(self.B[i % 4][:, :], kT[:, kt * 128:(kt + 1) * 128], qo3()[:, h, qb * 512:(qb + 1) * 512], True, True,
                            [("kv", slot), ("qo", h, qb)], [("B", i % 4)], inc=True)
                    self.act(pT(i % 4), self.B[i % 4][:, :], AF.Exp, [("B", i % 4), "nbias"], [("pT", i % 4)],
                             scale=scale, bias=self.nbias[:, a:a + 1])

                def OLstep(i):
                    hq, qb, kt = steps[i]
                    h = g * 4 + hq
                    blk = i // 64
                    bo, bl = self.B[4 + blk % 2], self.B[6 + blk % 2]
                    self.mm(bo[:, :], v3[:, kt, :], pT(i % 4), kt == 0, kt == 63, [("kv", slot), ("pT", i % 4)], [("B", 4 + blk % 2)], inc=False)
                    self.mm(bl[:, :], self.ones_b[:, :], pT(i % 4), kt == 0, kt == 63, [("pT", i % 4), "ones_b"], [("B", 6 + blk % 2)], inc=True)
                    if kt == 63:
                        r_ = rl(blk % 2)
                        self.recip(r_, bl[:, :], [("B", 6 + blk % 2)], [("rl", blk % 2)])
                        self.tt("dve", qo3()[:, h, qb * 512:(qb + 1) * 512], bo[:, :], r_, ALU.mult,
                                [("B", 4 + blk % 2), ("rl", blk % 2)], [("qo", h, qb)])
                for i in range(n + SK):
                    if i < n:
                        Sstep(i)
                    if i >= SK:
                        OLstep(i - SK)
            items.append(Item(core, pool="kv", src=lambda g=g: kvsrc(g), ld_reads=["kall", "vall"]))

        for m in range(KC):
            def oproj(slot, tile, m=m):
                w3 = tile.rearrange("p (k c) -> p k c", c=128)
                b0 = (m % 2) * 2
                for k in range(KC):
                    for th in range(2):
                        self.mm(self.B[b0 + th][:, :], w3[:, k, :], qo3()[:, k, th * 512:(th + 1) * 512], k == 0, k == KC - 1,
                                [("w128", slot), ("qo", k, th)], [("B", b0 + th)], inc=(k == KC - 1))
                for th in range(2):
                    xs = self.x3[:, m, th * 512:(th + 1) * 512]
                    self.tt("dve", xs, self.B[b0 + th][:, :], xs, ALU.add, [("B", b0 + th), ("x", m, th)], [("x", m, th)])
            items.append(Item(oproj, pool="w128", src=lambda m=m: [(lambda tl: tl, wo().ap()[m])]))
        items.append(Item(None, kind="fence"))


    def rec_items(self, r, items):
        gi = 10 + r
        A_GATE, A_W128, A_GW, A_HAL, A_HALS, A_CAR, A_CALS, A_HL, A_HR, A_HIN, A_ST = \
            0, 32768, 45056, 48128, 48384, 49920, 50176, 52224, 52352, 52480, 52608
        gate3 = lambda: self.carve(A_GATE, 32768, BF16).rearrange("p (c t) -> p c t", t=T)
        hal = lambda: self.carve(A_HAL, 192, F32).rearrange("p (c w) -> p c w", w=3)
        hals = lambda: self.carve(A_HALS, 1536, F32).rearrange("p (r c w) -> p r c w", c=16, w=3)
        car = lambda: self.carve(A_CAR, 256, F32)
        cals = lambda: self.carve(A_CALS, 2048, F32).rearrange("p (r c) -> p r c", c=64)
        hl = lambda: self.carve(A_HL, 128, F32).rearrange("p (c w) -> p c w", w=2)
        hr = lambda: self.carve(A_HR, 64, F32).rearrange("p (c w) -> p c w", w=1)
        hin = lambda: self.carve(A_HIN, 128, F32)
        st = lambda: self.carve(A_ST, 128, F32)
        xnf = self.xn[:, :].bitcast(F32)
        Wt = lambda dr, i: xnf[:, (dr * 3 + i) * 512:(dr * 3 + i + 1) * 512]
        xcb = lambda i: self.xn[:, 6144 + i * 1024:6144 + (i + 1) * 1024]
        Hs = lambda dr: xnf[:, 4096 + dr * 1024:4096 + (dr + 1) * 1024]
        ctmp = xnf[:, 6144:7168]
        x3 = self.x3
        sm2 = self.sm2
        cdec = lambda dr, c, which: sm2[:, which * 32 + dr * 16 + c:which * 32 + dr * 16 + c + 1]
        sumr = lambda dr, c, th: sm2[:, 64 + (dr * 16 + c) * 2 + th:64 + (dr * 16 + c) * 2 + th + 1]
        XNALL = [("xn", k, th) for k in range(KC) for th in range(2)]

        def setup(_):
            self.make_pool("w128", A_W128, 4096, 3)
            self.rmsnorm(gi)
            xsp = self.dscr(f"xsp_{r}", [128, KC * T], F32, "w")
            for k in range(KC):
                self.dma("sp", xsp.ap()[:, k * T:(k + 1) * T], x3[:, k, :], [("x", k, 0), ("x", k, 1)], [("xsp", k)], f"stx{k % 4}",
                         out_flag=not self.fused)
            lam = self.cv(C_LAM + r * 32, 32)
            t0, t1, t2 = sm2[:, 128:160], sm2[:, 160:192], sm2[:, 192:224]
            self.ts("dve", t2, lam, -1.0, None, ALU.mult, None, ["cvec"], ["dc2"])
            self.tt("dve", t0, lam, t2, ALU.max, ["cvec", "dc2"], ["dc0"])
            self.act(t1, t0, AF.Exp, ["dc0"], ["dc1"], scale=-1.0)
            self.act(t1, t1, AF.Ln, ["dc1", "cvec"], ["dc1"], scale=1.0, bias=self.cv(C_ONE))
            self.P.op("dve", lambda e: e.tensor_scalar_max(out=t2, in0=t2, scalar1=0.0), ["dc2"], ["dc2"])
            self.tt("dve", t1, t1, t2, ALU.add, ["dc1", "dc2"], ["dc1"])
            self.ts("dve", sm2[:, 0:32], t1, -8.0, None, ALU.mult, None, ["dc1"], ["cdec"])
            self.ts("dve", sm2[:, 32:64], t1, -16.0, None, ALU.mult, None, ["dc1"], ["cdec2"])
        items.append(Item(setup))

        win = lambda: self.din(f"win_{r}", [32, 128, KC * 128])
        wout = lambda: self.din(f"wout_{r}", [KC, 128, KC * 128])
        gwd = lambda: self.din(f"gw_{r}", [KC, 128, 512])

        for c32 in list(range(16, 32)) + list(range(0, 16)):
            def inproj(slot, tile, c32=c32):
                w3 = tile.rearrange("p (k c) -> p k c", c=128)
                c = c32 % 16
                b0 = (c % 2) * 2
                for k in range(KC):
                    for th in range(2):
                        self.mm(self.B[b0 + th][:, :], w3[:, k, :], self.xn3[:, k, th * 512:(th + 1) * 512], k == 0, k == KC - 1,
                                [("w128", slot), ("xn", k, th)], [("B", b0 + th)], inc=(k == KC - 1))
                for th in range(2):
                    tsl = slice(th * 512, (th + 1) * 512)
                    if c32 >= 16:
                        self.cp("act", x3[:, c, tsl], self.B[b0 + th][:, :], [("B", b0 + th)], [("x", c, th)])
                    else:
                        self.act(gate3()[:, c, tsl], self.B[b0 + th][:, :], AF.Gelu_apprx_tanh, [("B", b0 + th)], [("gate", c, th)])
            items.append(Item(inproj, pool="w128", src=lambda c32=c32: [(lambda tl: tl, win().ap()[c32])]))

        def halo_out(_):
            allx = [("x", k, th) for k in range(KC) for th in range(2)]
            self.cp("dve", hal()[:, :, 0:1], x3[:, :, 0:1], allx, ["hal"])
            self.cp("dve", hal()[:, :, 1:3], x3[:, :, T - 2:T], allx, ["hal"])
            hin_d = self.dscr(f"hin_{r}", [128, 48], F32, "w")
            self.dma("sp", hin_d.ap(), self.carve(A_HAL, 192, F32), ["hal"], ["hin"], "xo0", out_flag=not self.fused)
        items.append(Item(halo_out))
        items.append(Item(None, kind="cut", cid=f"r{r}a",
                          gathers=[(f"hin_{r}", [128, 48], f"hall_{r}", [NCORES * 128, 48], F32, "hin", "hall")],
                          save=["x", "gate"]))

        def halo_in(_):
            self.make_pool("gw", A_GW, 1024, 3)
            self.make_pool("w128", A_W128, 4096, 3)
            hall = self.dscr(f"hall_{r}", [NCORES * 128, 48], F32, "r")
            self.dma("sp", self.carve(A_HALS, 1536, F32).rearrange("p (r c) -> p r c", c=48),
                     hall.ap().rearrange("(r p) c -> p r c", p=128), ["hall"], ["hals"], "cst")
            for rr in range(NCORES):
                mp = self.cv(C_MASK + rr)
                mn = self.cv(C_MASK + 8 + rr)
                if rr == 0:
                    self.ts("dve", hl(), hals()[:, rr, :, 1:3], mp, None, ALU.mult, None, ["hals", "cvec"], ["hl"])
                    self.ts("dve", hr(), hals()[:, rr, :, 0:1], mn, None, ALU.mult, None, ["hals", "cvec"], ["hr"])
                else:
                    self.stt("dve", hl(), hals()[:, rr, :, 1:3], mp, hl(), ALU.mult, ALU.add, ["hals", "cvec", "hl"], ["hl"])
                    self.stt("dve", hr(), hals()[:, rr, :, 0:1], mn, hr(), ALU.mult, ALU.add, ["hals", "cvec", "hr"], ["hr"])
            for c in range(KC):
                cw = lambda tap: self.cv(C_CONV + r * 80 + c * 5 + tap)
                rc = x3[:, c, :]
                RX = [("x", c, 0), ("x", c, 1)]
                CT = [("xn", 12, 0), ("xn", 12, 1), ("xn", 13, 0), ("xn", 13, 1)]
                self.ts("dve", ctmp, rc, cw(2), cw(4), ALU.mult, ALU.add, RX + ["cvec"], CT)
                self.stt("dve", ctmp[:, 1:T], rc[:, 0:T - 1], cw(1), ctmp[:, 1:T], ALU.mult, ALU.add, RX + CT, CT)
                self.stt("dve", ctmp[:, 2:T], rc[:, 0:T - 2], cw(0), ctmp[:, 2:T], ALU.mult, ALU.add, RX + CT, CT)
                self.stt("dve", ctmp[:, 0:T - 1], rc[:, 1:T], cw(3), ctmp[:, 0:T - 1], ALU.mult, ALU.add, RX + CT, CT)
                self.stt("dve", ctmp[:, 0:1], hl()[:, c, 1:2], cw(1), ctmp[:, 0:1], ALU.mult, ALU.add, ["hl"] + CT, CT)
                self.stt("dve", ctmp[:, 0:2], hl()[:, c, 0:2], cw(0), ctmp[:, 0:2], ALU.mult, ALU.add, ["hl"] + CT, CT)
                self.stt("dve", ctmp[:, T - 1:T], hr()[:, c, 0:1], cw(3), ctmp[:, T - 1:T], ALU.mult, ALU.add, ["hr"] + CT, CT)
                self.cp("pool", rc, ctmp, CT, RX)
        items.append(Item(halo_in))

        def lru_pass(slot, gw, c, final):
            gw3 = gw.rearrange("p (m o) -> p m o", o=128)
            xb = xcb(c % 2)
            XB = [("xn", 6 + (c % 2), 0), ("xn", 6 + (c % 2), 1)]
            RX = [("x", c, 0), ("x", c, 1)]
            self.cp("act", xb, x3[:, c, :], RX, XB)
            for dr in range(2):
                W1, W2, W3 = Wt(dr, 0), Wt(dr, 1), Wt(dr, 2)
                RW = lambda i: [("xn", (dr * 3 + i), 0), ("xn", (dr * 3 + i), 1)]
                Hd = Hs(dr)
                HR = [("xn", 8 + dr * 2, 0), ("xn", 8 + dr * 2, 1), ("xn", 9 + dr * 2, 0), ("xn", 9 + dr * 2, 1)]
                order = [0, 1] if dr == 0 else [1, 0]
                for n_, th in enumerate(order):
                    tsl = slice(th * 512, (th + 1) * 512)
                    br, bi = self.B[dr * 4 + th * 2], self.B[dr * 4 + th * 2 + 1]
                    self.mm(br[:, :], gw3[:, dr * 2 + 0, :], xb[:, tsl], True, True, [("gw", slot)] + XB, [("B", dr * 4 + th * 2)], inc=True)
                    self.mm(bi[:, :], gw3[:, dr * 2 + 1, :], xb[:, tsl], True, True, [("gw", slot)] + XB, [("B", dr * 4 + th * 2 + 1)], inc=True)
                    gb = lambda z: self.cv(C_GB + r * 64 + dr * 32 + z * 16 + c)
                    if final:
                        self.act(W1, br[:, :], AF.Sigmoid, [("B", dr * 4 + th * 2), "cvec"], RW(0), bias=gb(0))
                    else:
                        self.act(W1, br[:, :], AF.Sigmoid, [("B", dr * 4 + th * 2), "cvec"], RW(0) + [("sumr", dr, c, th)], bias=gb(0),
                                 accum=sumr(dr, c, th))
                    self.act(W2, bi[:, :], AF.Sigmoid, [("B", dr * 4 + th * 2 + 1), "cvec"], RW(1), bias=gb(1))
                    self.act(W3, W1, AF.Exp, RW(0) + ["cdec2"], RW(2), scale=cdec(dr, c, 1))
                    self.act(W3, W3, AF.Sqrt, RW(2) + ["cvec"], RW(2), scale=-1.0, bias=self.cv(C_ONE))
                    self.act(W1, W1, AF.Exp, RW(0) + ["cdec"], RW(0), scale=cdec(dr, c, 0))
                    self.tt("dve", W2, W2, x3[:, c, tsl], ALU.mult, RW(1) + [("x", c, th)], RW(1))
                    self.tt("dve", W2, W2, W3, ALU.mult, RW(1) + RW(2), RW(1))
                    if n_ == 0:
                        init = hin()[:, dr * 16 + c:dr * 16 + c + 1] if final else 0.0
                        ir = ["hin"] if final else []
                    else:
                        init = Hd[:, 511:512] if dr == 0 else Hd[:, 512:513]
                        ir = []
                    if dr == 0:
                        o_, a_, b_ = Hd[:, tsl], W1, W2
                    else:
                        o_, a_, b_ = Hd[:, tsl][:, ::-1], W1[:, ::-1], W2[:, ::-1]
                    self.P.op("dve", lambda e, o_=o_, a_=a_, b_=b_, init=init: e.tensor_tensor_scan(
                        out=o_, data0=a_, data1=b_, initial=init, op0=ALU.mult, op1=ALU.add), RW(0) + RW(1) + HR + ir, HR)
                if not final:
                    col = car()[:, dr * 16 + c:dr * 16 + c + 1]
                    src = Hd[:, T - 1:T] if dr == 0 else Hd[:, 0:1]
                    self.cp("dve", col, src, HR, [("car", dr, c)])
            if final:
                HR0 = [("xn", 8, 0), ("xn", 8, 1), ("xn", 9, 0), ("xn", 9, 1)]
                HR1 = [("xn", 10, 0), ("xn", 10, 1), ("xn", 11, 0), ("xn", 11, 1)]
                self.tt("pool", Hs(0), Hs(0), Hs(1), ALU.add, HR0 + HR1, HR0)
                self.tt("dve", gate3()[:, c, :], gate3()[:, c, :], Hs(0), ALU.mult, HR0 + [("gate", c, 0), ("gate", c, 1)],
                        [("gate", c, 0), ("gate", c, 1)])
                xsp = self.dscr(f"xsp_{r}", [128, KC * T], F32, "r")
                self.dma("sp", x3[:, c, :], xsp.ap()[:, c * T:(c + 1) * T], [("xsp", c)], RX, f"ldx{c % 4}")

        gwsrc = lambda c: [(lambda tl: tl, gwd().ap()[c])]
        for c in range(KC):
            items.append(Item(lambda slot, tile, c=c: lru_pass(slot, tile, c, False), pool="gw", src=lambda c=c: gwsrc(c)))

        def carry_out(_):
            s2 = sm2[:, 64:128].rearrange("p (n t) -> p n t", t=2)
            SR = [("sumr", dr, c, th) for dr in range(2) for c in range(KC) for th in range(2)]
            self.tt("dve", st(), s2[:, :, 0], s2[:, :, 1], ALU.add, SR, ["st"])
            self.tt("dve", st(), st(), sm2[:, 0:32], ALU.mult, ["st", "cdec"], ["st"])
            self.act(car()[:, 32:64], st(), AF.Exp, ["st"], ["carA"])
            cin = self.dscr(f"cin_{r}", [128, 64], F32, "w")
            self.dma("sp", cin.ap(), car(), ["carA"] + [("car", dr, c) for dr in range(2) for c in range(KC)], ["cin"], "xo0",
                     out_flag=not self.fused)
        items.append(Item(carry_out))
        items.append(Item(None, kind="cut", cid=f"r{r}b",
                          gathers=[(f"cin_{r}", [128, 64], f"call_{r}", [NCORES * 128, 64], F32, "cin", "call")],
                          save=["x", "gate"]))

        def carry_in(_):
            self.make_pool("gw", A_GW, 1024, 3)
            self.make_pool("w128", A_W128, 4096, 3)
            call = self.dscr(f"call_{r}", [NCORES * 128, 64], F32, "r")
            self.dma("sp", cals(), call.ap().rearrange("(r p) c -> p r c", p=128), ["call"], ["cals"], "cst")
            am, hm = st()[:, 0:16], st()[:, 16:32]
            for dr in range(2):
                state = hin()[:, dr * 16:(dr + 1) * 16]
                self.P.op("dve", lambda e, state=state: e.memset(state, 0.0), [], [("hin", dr)])
                order = range(NCORES) if dr == 0 else range(NCORES - 1, -1, -1)
                for rr in order:
                    m_ = self.cv(C_MASK + (16 if dr == 0 else 32) + rr)
                    om = self.cv(C_MASK + (24 if dr == 0 else 40) + rr)
                    self.ts("dve", am, cals()[:, rr, 32 + dr * 16:48 + dr * 16], m_, om, ALU.mult, ALU.add, ["cals", "cvec"], ["am"])
                    self.ts("dve", hm, cals()[:, rr, dr * 16:(dr + 1) * 16], m_, None, ALU.mult, None, ["cals", "cvec"], ["hm"])
                    self.tt("dve", state, state, am, ALU.mult, [("hin", dr), "am"], [("hin", dr)])
                    self.tt("dve", state, state, hm, ALU.add, [("hin", dr), "hm"], [("hin", dr)])
            self.cp("dve", st()[:, 0:1], hin()[:, 0:1], [("hin", 0), ("hin", 1), "am", "hm"], ["hin"])
        items.append(Item(carry_in))

        for c in range(KC):
            items.append(Item(lambda slot, tile, c=c: lru_pass(slot, tile, c, True), pool="gw", src=lambda c=c: gwsrc(c)))

        for m in range(KC):
            def oproj(slot, tile, m=m):
                w3 = tile.rearrange("p (k c) -> p k c", c=128)
                b0 = (m % 2) * 2
                for k in range(KC):
                    for th in range(2):
                        self.mm(self.B[b0 + th][:, :], w3[:, k, :], gate3()[:, k, th * 512:(th + 1) * 512], k == 0, k == KC - 1,
                                [("w128", slot), ("gate", k, th)], [("B", b0 + th)], inc=(k == KC - 1))
                for th in range(2):
                    xs = x3[:, m, th * 512:(th + 1) * 512]
                    self.tt("dve", xs, self.B[b0 + th][:, :], xs, ALU.add, [("B", b0 + th), ("x", m, th)], [("x", m, th)])
            items.append(Item(oproj, pool="w128", src=lambda m=m: [(lambda tl: tl, wout().ap()[m])]))
        items.append(Item(None, kind="fence"))

    def final_items(self, items):
        def fin(_):
            x3 = self.x3
            for th in range(2):
                tsl = slice(th * 512, (th + 1) * 512)
                bank = self.B[6 + th]
                for k in range(KC):
                    sq = self.sq[k % 2]
                    self.act(sq[:, :], x3[:, k, tsl], AF.Square, [("x", k, th)], [("sq", k % 2)])
                    self.mm(bank[:, :], self.ones_f[:, :], sq[:, :], k == 0, k == KC - 1,
                            [("sq", k % 2), "ones_f"], [("B", 6 + th)], inc=True)
                rt = self.rs[th]
                self.act(rt[:, :], bank[:, :], AF.Sqrt, [("B", 6 + th), "cvec"], [("rs", th)],
                         scale=1.0 / D, bias=self.cv(C_EPS))
                self.recip(rt[:, :], rt[:, :], [("rs", th)], [("rs", th)])
                for k in range(KC):
                    self.stt("dve", x3[:, k, tsl], x3[:, k, tsl], self.cv(C_GAIN + 12 * 16 + k), rt[:, :],
                             ALU.mult, ALU.mult, [("x", k, th), ("rs", th), "cvec"], [("x", k, th)])
        items.append(Item(fin))

    def xres(self):
        return [("x", k, th) for k in range(KC) for th in range(2)]

    def load_x(self, name):
        src = self.din(name, [128, KC * T])
        for k in range(KC):
            self.dma("sp", self.x3[:, k, :], src.ap()[:, k * T:(k + 1) * T], [], [("x", k, 0), ("x", k, 1)], f"ldx{k % 4}")

    def store_x(self, name, final=True):
        dst = self.dout(name, [128, KC * T])
        for k in range(KC):
            self.dma("sp", dst.ap()[:, k * T:(k + 1) * T], self.x3[:, k, :], [("x", k, 0), ("x", k, 1)], [], f"stx{k % 4}",
                     out_flag=True)

    def run_items(self, items):
        P = self.P
        n = len(items)
        loaded = {}
        nxt = 0
        LA = 6

        def try_load(upto, cur):
            nonlocal nxt
            while nxt < n and nxt <= upto:
                it = items[nxt]
                if it.kind != "compute":
                    if nxt > cur:
                        return
                    nxt += 1
                    continue
                if it.pool is None:
                    nxt += 1
                    continue
                if it.pool not in self.pools:
                    if nxt > cur:
                        return
                p = self.pools[it.pool]
                sidx = p["cnt"]
                slot = sidx % p["n"]
                prev = p["last_item"].get(slot)
                if prev is not None and prev >= cur:
                    return
                tile = self.slot_view(it.pool, slot)
                for (vf, src) in it.src():
                    dst = vf(tile)
                    self.dma(p["q"], dst, src, it.kw.get("ld_reads", []), [(it.pool, slot)], f"{it.pool}{slot}",
                             **it.kw.get("dma_kw", {}))
                p["cnt"] += 1
                p["last_item"][slot] = nxt
                loaded[nxt] = (slot, tile)
                nxt += 1

        for i, it in enumerate(items):
            if it.kind == "fence":
                assert nxt >= i
                P.fence(lambda e: e.activation(out=self.small[:, 0:1], in_=self.small[:, 1:2], func=AF.Copy))
                self.pools = {}
                nxt = max(nxt, i + 1)
                continue
            if it.kind == "cut":
                assert self.fused
                for (in_name, in_shape, out_name, out_shape, dt, rin, rout) in it.kw["gathers"]:
                    i_t = self.dscr(in_name, in_shape, dt, "w")
                    o_t = self.dscr(out_name, out_shape, dt, "r")
                    P.op("pool", lambda e, i_t=i_t, o_t=o_t: e.collective_compute(
                        "AllGather", ALU.bypass, replica_groups=[list(range(NCORES))],
                        ins=[i_t.ap().opt()], outs=[o_t.ap().opt()]), [rin], [rout], dma="cc", dinc=1)
                P.fence(lambda e: e.activation(out=self.small[:, 0:1], in_=self.small[:, 1:2], func=AF.Copy))
                self.pools = {}
                nxt = max(nxt, i + 1)
                continue
            if it.pool is None:
                it.fn(None)
                try_load(i + LA, i)
            else:
                try_load(i, i)
                assert i in loaded, f"item {i} pool {it.pool} not loaded"
                slot, tile = loaded.pop(i)
                try_load(i + LA, i)
                it.fn(slot, tile)


def colchunk(W):
    K, N = W.shape
    return np.ascontiguousarray(W.reshape(K // 128, 128, N // 128, 128).transpose(2, 1, 0, 3)).reshape(N // 128, 128, (K // 128) * 128)


def prep_host(inp):
    H = {}
    for l in range(DEPTH):
        for f in range(2):
            w = inp["ffn_w_gu"][l, f]
            w = w.reshape(KC, 128, 2, JC, 128)
            H[f"wgu_{l}_{f}"] = np.ascontiguousarray(w.transpose(3, 1, 0, 2, 4)).reshape(JC, 128, KC * 256)
            w = inp["ffn_w_down"][l, f]
            w = w.reshape(NQ, JQ, 128, KC, 128)
            H[f"wd_{l}_{f}"] = np.ascontiguousarray(w.transpose(0, 3, 2, 1, 4)).reshape(NQ, KC, 128, JQ * 128)
    cvec = np.zeros((128, NCV), np.float32)

    def pk(v):
        return v.reshape(KC, 128).T
    for l in range(DEPTH):
        for f in range(2):
            gi = l * 2 + f
            cvec[:, C_GAIN + gi * 16:C_GAIN + gi * 16 + 16] = pk(inp["ffn_norm"][l, f])
    for a in range(2):
        cvec[:, C_GAIN + (8 + a) * 16:C_GAIN + (9 + a) * 16] = pk(inp["attn_norm"][a])
        cvec[:, C_GAIN + (10 + a) * 16:C_GAIN + (11 + a) * 16] = pk(inp["rec_norm"][a])
        cvec[:, C_AQK + a * 2 + 0] = inp["attn_q_norm"][a]
        cvec[:, C_AQK + a * 2 + 1] = inp["attn_k_norm"][a]
    cvec[:, C_GAIN + 12 * 16:C_GAIN + 13 * 16] = pk(inp["final_norm"])
    cvec[:, C_EPS] = EPS
    cvec[:, C_ONE] = 1.0
    H["cvec"] = cvec
    R = np.zeros((128, 128), np.float32)
    for d in range(128):
        if d % 64 < 32:
            R[d, d + 32] = -1.0
        else:
            R[d, d - 32] = 1.0
    H["prot"] = np.ascontiguousarray(R.T)
    for a in range(2):
        wq = inp["attn_w_qkv"][a]
        H[f"wqk_{a}"] = colchunk(wq[:, 0:2560])
        wv = wq[:, 2560:3072].reshape(KC, 128, 2, 256)
        H[f"wv_{a}"] = np.ascontiguousarray(wv.transpose(2, 1, 0, 3)).reshape(2, 128, KC * 256)
        H[f"wo_{a}"] = colchunk(inp["attn_w_o"][a])
    for r in range(2):
        H[f"win_{r}"] = colchunk(inp["rec_w_in"][r])
        H[f"wout_{r}"] = colchunk(inp["rec_w_out"][r])
        gw = inp["rec_gate_w"][r]
        H[f"gw_{r}"] = np.ascontiguousarray(gw.transpose(2, 3, 0, 1, 4)).reshape(KC, 128, 512)
        for c in range(KC):
            for tap in range(4):
                cvec[:, C_CONV + r * 80 + c * 5 + tap] = inp["rec_conv_w"][r, tap, c * 128:(c + 1) * 128]
            cvec[:, C_CONV + r * 80 + c * 5 + 4] = inp["rec_conv_b"][r, c * 128:(c + 1) * 128]
            for dr in range(2):
                for z in range(2):
                    cvec[:, C_GB + r * 64 + dr * 32 + z * 16 + c] = inp["rec_gate_b"][r, dr, z, c * 128:(c + 1) * 128]
                cvec[:, C_LAM + r * 32 + dr * 16 + c] = inp["rec_lambda"][r, dr, c * 128:(c + 1) * 128]
    H["arow"] = np.stack([np.concatenate([inp["attn_q_norm"][a], inp["attn_k_norm"][a]]) for a in range(2)]).astype(np.float32)
    inv_freq = (np.float32(10000.0) ** (-np.arange(32, dtype=np.float32) / np.float32(32))).astype(np.float32)
    tok = np.arange(S)
    rows = (tok // 64).astype(np.float32)
    cols = (tok % 64).astype(np.float32)
    ang_row = rows[:, None] * inv_freq[None, :]
    ang_col = cols[:, None] * inv_freq[None, :]
    ang = np.zeros((S, 128), np.float32)
    ang[:, 0:32] = ang_row
    ang[:, 32:64] = ang_row
    ang[:, 64:96] = ang_col
    ang[:, 96:128] = ang_col
    cosT = np.cos(ang).astype(np.float32).T
    sinT = np.sin(ang).astype(np.float32).T
    H["percore"] = []
    for c in range(NCORES):
        pc = {}
        pc["ctab"] = np.ascontiguousarray(np.concatenate([cosT[:, c * T:(c + 1) * T], sinT[:, c * T:(c + 1) * T]], axis=1))
        cvc = cvec.copy()
        mk = np.zeros(48, np.float32)
        for r in range(NCORES):
            mk[r] = 1.0 if r == c - 1 else 0.0
            mk[8 + r] = 1.0 if r == c + 1 else 0.0
            mk[16 + r] = 1.0 if r < c else 0.0
            mk[24 + r] = 0.0 if r < c else 1.0
            mk[32 + r] = 1.0 if r > c else 0.0
            mk[40 + r] = 0.0 if r > c else 1.0
        cvc[:, C_MASK:C_MASK + 48] = mk[None, :]
        pc["cvec"] = cvc
        H["percore"].append(pc)
    return H


def x_to_dev(x, c):
    xs = x[0, c * T:(c + 1) * T, :]
    return np.ascontiguousarray(xs.T.reshape(KC, 128, T).transpose(1, 0, 2)).reshape(128, KC * T)


def x_from_dev(a):
    return a.reshape(128, KC, T).transpose(2, 1, 0).reshape(T, D)


LAST_N_LAUNCH = 0
TRACE = False
EXEC_NS = []


def make_items(bld, phases):
    items = []
    for ph in phases:
        p = ph.split(":")
        if p[0] == "ffn":
            bld.ffn_items(int(p[1]), int(p[2]), items)
        elif p[0] == "attn":
            bld.attn_items(int(p[1]), items)
        elif p[0] == "rec":
            bld.rec_items(int(p[1]), items)
        elif p[0] == "final":
            bld.final_items(items)
    return items


def all_phases():
    phases = []
    for l in range(DEPTH):
        phases.append(f"ffn:{l}:0")
        phases.append(f"attn:{l // 2}" if l % 2 == 0 else f"rec:{l // 2}")
        phases.append(f"ffn:{l}:1")
    phases.append("final")
    return phases


def state_spec(bld, name):
    if name == "x":
        return bld.xT[:, :], KC * T, F32, [[("x", k, 0), ("x", k, 1)] for k in range(KC)], KC
    if name == "qo":
        return bld.carve(0, 32768, BF16), KC * T, BF16, [[("qo", k, 0), ("qo", k, 1)] for k in range(KC)], KC
    if name == "gate":
        return bld.carve(0, 32768, BF16), KC * T, BF16, [[("gate", k, 0), ("gate", k, 1)] for k in range(KC)], KC
    raise KeyError(name)


def build_segment(phases, seg, fused):
    bld = Builder(fused=fused)
    items = make_items(bld, phases)
    cuts = [i for i, it in enumerate(items) if it.kind == "cut"]
    bld.load_consts()
    if fused:
        lo, hi, cin, cout = 0, len(items), None, None
    else:
        bounds = [-1] + cuts + [len(items)]
        lo, hi = bounds[seg] + 1, bounds[seg + 1]
        cin = items[bounds[seg]] if seg > 0 else None
        cout = items[bounds[seg + 1]] if seg + 1 < len(bounds) - 1 else None
    if cin is None:
        bld.load_x("x_in")
    else:
        for name in cin.kw["save"]:
            ap, n, dt, rl, nch = state_spec(bld, name)
            src = bld.din(f"st_{cin.kw['cid']}_{name}", [128, n], dt)
            w = n // nch
            for i in range(nch):
                bld.dma("sp", ap[:, i * w:(i + 1) * w], src.ap()[:, i * w:(i + 1) * w], [], rl[i], f"ldx{i % 4}")
    bld.run_items(items[lo:hi])
    if cout is None:
        bld.store_x("x_out")
    else:
        for name in cout.kw["save"]:
            ap, n, dt, rl, nch = state_spec(bld, name)
            dst = bld.dout(f"st_{cout.kw['cid']}_{name}", [128, n], dt)
            w = n // nch
            for i in range(nch):
                bld.dma("sp", dst.ap()[:, i * w:(i + 1) * w], ap[:, i * w:(i + 1) * w], rl[i], [], f"stx{i % 4}", out_flag=True)
    nc = bld.P.finish()
    nseg = 1 if fused else len(cuts) + 1
    return bld, nc, cout, nseg


def run_prog(bld, nc, H, store):
    in_maps = []
    for c in range(NCORES):
        m = {}
        for name in bld.in_decl:
            if name in store[c]:
                m[name] = store[c][name]
            else:
                m[name] = H[name]
        in_maps.append(m)
    res = run_bass_kernel_spmd(nc, in_maps, core_ids=list(range(NCORES)), trace=TRACE)
    if TRACE:
        print("exec_time_ns", res.exec_time_ns)
        EXEC_NS.append(res.exec_time_ns)
    return res


def run_phases(H, phases, x0, fused):
    store = [{"x_in": x_to_dev(x0, c)} for c in range(NCORES)]
    for c in range(NCORES):
        store[c].update(H["percore"][c])
    seg = 0
    while True:
        bld, nc, cout, nseg = build_segment(phases, None if fused else seg, fused)
        res = run_prog(bld, nc, H, store)
        for c in range(NCORES):
            store[c].update(res.results[c])
        if cout is not None:
            for (in_name, in_shape, out_name, out_shape, dt, rin, rout) in cout.kw["gathers"]:
                full = np.concatenate([res.results[c][in_name] for c in range(NCORES)], axis=0)
                for c in range(NCORES):
                    store[c][out_name] = full
        seg += 1
        if fused or seg >= nseg:
            break
    return np.concatenate([x_from_dev(store[c]["x_out"]) for c in range(NCORES)], 0)


FUSED = False


def kernel(**inp):
    inp = {k: np.asarray(v) for k, v in inp.items()}
    H = prep_host(inp)
    out = run_phases(H, all_phases(), inp["x"], FUSED)
    return out.reshape(1, S, D).astype(np.float32)
```
